# Optimizing a Trainium2 kernel written in Bass

```python
import jax, jax.numpy as jnp
from jax import lax
import numpy as np

D_MODEL = 2048
BATCH = 4
SEQ = 2048
DEPTH = 2

GRID_W = 64
CTX_LEN = 256
N_MIXERS = 2
N_HEADS = 16
QK_NOPE = 128
QK_ROPE = 64
V_DIM = 128
Q_LORA = 512
KV_LORA = 512
ROPE_THETA = 10000.0
ATTN_SCALE = (QK_NOPE + QK_ROPE) ** -0.5
Q_BLOCK = 128
CONV_WIDTH = 31
D_FF = 5632
FFN_RES_WEIGHT = 0.5
N_MOD = 9
EPS = 1e-6
N_ATTN_LAYERS = (DEPTH + N_MIXERS - 1) // N_MIXERS
N_CONV_LAYERS = DEPTH // N_MIXERS

kernel_name = "hybrid_mla_conformer_dit_prefix"

F32 = jnp.float32


def rms_norm(x, g):
    xf = x.astype(F32)
    y = xf * lax.rsqrt(jnp.mean(xf * xf, axis=-1, keepdims=True) + EPS)
    return (y * g.astype(F32)).astype(x.dtype)


def layer_norm(x, g, b):
    xf = x.astype(F32)
    mu = jnp.mean(xf, axis=-1, keepdims=True)
    xc = xf - mu
    var = jnp.mean(xc * xc, axis=-1, keepdims=True)
    y = xc * lax.rsqrt(var + EPS)
    return (y * g.astype(F32) + b.astype(F32)).astype(x.dtype)


def ada_mod(cond, w, b):
    cond2 = cond.reshape(-1, cond.shape[-1])
    m = jax.nn.silu(cond2) @ w + b
    return jnp.split(m[:, None, :], N_MOD, axis=-1)


def norm_mod(h, g, shift, scale):
    return rms_norm(h, g) * (1 + scale) + shift


def swiglu(h, w1, w3, w2):
    return (jax.nn.silu(h @ w1) * (h @ w3)) @ w2


def axial_rope_tables(n_tokens):
    t = jnp.arange(n_tokens, dtype=jnp.int32)
    row = (t // GRID_W).astype(F32)
    col = (t % GRID_W).astype(F32)
    n_axis = QK_ROPE // 4
    freqs = ROPE_THETA ** (-jnp.arange(n_axis, dtype=F32) / n_axis)
    ang = jnp.concatenate([row[:, None] * freqs, col[:, None] * freqs], axis=-1)
    return jnp.cos(ang)[:, None, :], jnp.sin(ang)[:, None, :]


def apply_rope(x, cos, sin):
    xr = x.astype(F32).reshape(x.shape[:-1] + (QK_ROPE // 2, 2))
    x1, x2 = xr[..., 0], xr[..., 1]
    out = jnp.stack([x1 * cos - x2 * sin, x1 * sin + x2 * cos], axis=-1)
    return out.reshape(x.shape).astype(x.dtype)


def mla_qkv(h, wdq, gq, wuq, wdkv, gkv, wukv, rope):
    b, t, _ = h.shape
    cq = rms_norm(h @ wdq, gq)
    q = (cq @ wuq).reshape(b, t, N_HEADS, QK_NOPE + QK_ROPE)
    q_nope, q_pe = q[..., :QK_NOPE], q[..., QK_NOPE:]
    kv = h @ wdkv
    ckv = rms_norm(kv[..., :KV_LORA], gkv)
    k_pe = kv[..., KV_LORA:][:, :, None, :]
    if rope is not None:
        q_pe = apply_rope(q_pe, *rope)
        k_pe = apply_rope(k_pe, *rope)
    kvu = (ckv @ wukv).reshape(b, t, N_HEADS, QK_NOPE + V_DIM)
    k_nope, v = kvu[..., :QK_NOPE], kvu[..., QK_NOPE:]
    q = jnp.concatenate([q_nope, q_pe], axis=-1)
    k = jnp.concatenate([k_nope, jnp.broadcast_to(k_pe, (b, t, N_HEADS, QK_ROPE))], axis=-1)
    return q, k, v


def attend(q, k, v):
    s = jnp.einsum('bqhd,bkhd->bhqk', q, k).astype(F32) * ATTN_SCALE
    p = jax.nn.softmax(s, axis=-1).astype(v.dtype)
    return jnp.einsum('bhqk,bkhd->bqhd', p, v)


def block_attention(q, k, v):
    b, t, h, dq = q.shape
    nb = t // Q_BLOCK
    qb = q.reshape(b, nb, Q_BLOCK, h, dq).transpose(1, 0, 2, 3, 4)
    out = lax.map(lambda qi: attend(qi, k, v), qb)
    return out.transpose(1, 0, 2, 3, 4).reshape(b, t, h, V_DIM)


def conv_module(h, w1, b1, dw, dwb, ln_g, ln_b, w2, b2):
    d = h.shape[-1]
    a = h @ w1 + b1
    u = a[..., :d] * jax.nn.sigmoid(a[..., d:])
    pad = CONV_WIDTH // 2
    u = lax.conv_general_dilated(u, dw[:, None, :].astype(u.dtype), window_strides=(1,),
                                 padding=[(pad, pad)], dimension_numbers=('NWC', 'WIO', 'NWC'),
                                 feature_group_count=d) + dwb
    u = jax.nn.silu(layer_norm(u, ln_g, ln_b))
    return u @ w2 + b2


def setup_inputs(seed: int = 0) -> dict:
    key = jax.random.key(seed)
    ks = jax.random.split(key, 32)

    def nrm(k, shape, fan_in, scale=1.0):
        return jax.random.normal(k, shape, F32) * (scale * fan_in ** -0.5)

    D = D_MODEL
    NA, NC = N_ATTN_LAYERS, N_CONV_LAYERS
    return {
        "x": jax.random.normal(ks[0], (BATCH, SEQ, D), F32),
        "c": jax.random.normal(ks[1], (BATCH, D), F32),
        "ctx": jax.random.normal(ks[2], (BATCH, CTX_LEN, D), F32),
        "c_ctx": jax.random.normal(ks[3], (D,), F32),
        "ada_w": nrm(ks[4], (DEPTH, D, N_MOD * D), D, 0.5),
        "ada_b": 0.02 * jax.random.normal(ks[5], (DEPTH, N_MOD * D), F32),
        "norm_g": 1.0 + 0.02 * jax.random.normal(ks[6], (DEPTH, 3, D), F32),
        "ffn_w1": nrm(ks[7], (DEPTH, 2, D, D_FF), D),
        "ffn_w3": nrm(ks[8], (DEPTH, 2, D, D_FF), D),
        "ffn_w2": nrm(ks[9], (DEPTH, 2, D_FF, D), D_FF),
        "mla_wdq": nrm(ks[10], (NA, D, Q_LORA), D),
        "mla_gq": 1.0 + 0.02 * jax.random.normal(ks[11], (NA, Q_LORA), F32),
        "mla_wuq": nrm(ks[12], (NA, Q_LORA, N_HEADS * (QK_NOPE + QK_ROPE)), Q_LORA),
        "mla_wdkv": nrm(ks[13], (NA, D, KV_LORA + QK_ROPE), D),
        "mla_gkv": 1.0 + 0.02 * jax.random.normal(ks[14], (NA, KV_LORA), F32),
        "mla_wukv": nrm(ks[15], (NA, KV_LORA, N_HEADS * (QK_NOPE + V_DIM)), KV_LORA),
        "mla_wo": nrm(ks[16], (NA, N_HEADS * V_DIM, D), N_HEADS * V_DIM),
        "conv_w1": nrm(ks[17], (NC, D, 2 * D), D),
        "conv_b1": 0.02 * jax.random.normal(ks[18], (NC, 2 * D), F32),
        "conv_dw": nrm(ks[19], (NC, CONV_WIDTH, D), CONV_WIDTH),
        "conv_dwb": 0.02 * jax.random.normal(ks[20], (NC, D), F32),
        "conv_ln_g": 1.0 + 0.02 * jax.random.normal(ks[21], (NC, D), F32),
        "conv_ln_b": 0.02 * jax.random.normal(ks[22], (NC, D), F32),
        "conv_w2": nrm(ks[23], (NC, D, D), D),
        "conv_b2": 0.02 * jax.random.normal(ks[24], (NC, D), F32),
        "final_g": 1.0 + 0.02 * jax.random.normal(ks[25], (D,), F32),
    }


def reference(x, c, ctx, c_ctx, ada_w, ada_b, norm_g, ffn_w1, ffn_w3, ffn_w2,
              mla_wdq, mla_gq, mla_wuq, mla_wdkv, mla_gkv, mla_wukv, mla_wo,
              conv_w1, conv_b1, conv_dw, conv_dwb, conv_ln_g, conv_ln_b, conv_w2, conv_b2,
              final_g):
    b, s, _ = x.shape
    rope = axial_rope_tables(s)
    h, hc = x, ctx
    for i in range(DEPTH):
        last = i == DEPTH - 1
        mixer = i % N_MIXERS
        j = i // N_MIXERS
        ctx_into_mixer = (not last) or mixer == 0
        ctx_out = not last
        m_l = ada_mod(c, ada_w[i], ada_b[i])
        m_c = ada_mod(c_ctx, ada_w[i], ada_b[i])

        def ffn_sub(hh, m, k, f):
            y = swiglu(norm_mod(hh, norm_g[i, k], m[3 * k], m[3 * k + 1]),
                       ffn_w1[i, f], ffn_w3[i, f], ffn_w2[i, f])
            return hh + FFN_RES_WEIGHT * m[3 * k + 2] * y

        h = ffn_sub(h, m_l, 0, 0)
        if ctx_into_mixer:
            hc = ffn_sub(hc, m_c, 0, 0)

        u_l = norm_mod(h, norm_g[i, 1], m_l[3], m_l[4])
        if mixer == 0:
            w = (mla_wdq[j], mla_gq[j], mla_wuq[j], mla_wdkv[j], mla_gkv[j], mla_wukv[j])
            u_c = norm_mod(hc, norm_g[i, 1], m_c[3], m_c[4])
            q_c, k_c, v_c = mla_qkv(u_c, *w, None)
            q_l, k_l, v_l = mla_qkv(u_l, *w, rope)
            k_all = jnp.concatenate([k_c, k_l], axis=1)
            v_all = jnp.concatenate([v_c, v_l], axis=1)
            o_l = block_attention(q_l, k_all, v_all).reshape(b, s, N_HEADS * V_DIM) @ mla_wo[j]
            h = h + m_l[5] * o_l
            if ctx_out:
                o_c = attend(q_c, k_c, v_c).reshape(b, hc.shape[1], N_HEADS * V_DIM) @ mla_wo[j]
                hc = hc + m_c[5] * o_c
        else:
            cw = (conv_w1[j], conv_b1[j], conv_dw[j], conv_dwb[j], conv_ln_g[j], conv_ln_b[j],
                  conv_w2[j], conv_b2[j])
            h = h + m_l[5] * conv_module(u_l, *cw)
            if ctx_out:
                u_c = norm_mod(hc, norm_g[i, 1], m_c[3], m_c[4])
                hc = hc + m_c[5] * conv_module(u_c, *cw)

        h = ffn_sub(h, m_l, 2, 1)
        if ctx_out:
            hc = ffn_sub(hc, m_c, 2, 1)
    return rms_norm(h, final_g)
```

```python
import numpy as np
import concourse.bass as bass
import concourse.mybir as mybir
from concourse.bass_utils import run_bass_kernel_spmd

F32 = mybir.dt.float32
BF16 = mybir.dt.bfloat16
AF = mybir.ActivationFunctionType
ALU = mybir.AluOpType

D = 2048
DFF = 5632
KC = 16
T = 1024
TC = 128
TT = T + TC
NH = 16
EPS = 1e-6
ATTN_SCALE = 192 ** -0.5
NSM = 1024
HALO = 16
CW = 31


def small_layout(jb):
    off = {}
    o = 0
    for name, n in (("normg", 96), ("finalg", 16), ("gq", 4), ("gkv", 4), ("b1", 32),
                    ("dw", 16 * CW), ("dwb", 16), ("lng", 16), ("lnb", 16), ("b2", 16),
                    ("sel", 16), ("adab", jb), ("cond", 80)):
        off[name] = o
        o += n
    assert o <= NSM
    return off


class Buf:
    __slots__ = ("name", "wtok", "rtoks", "dsem", "dcnt")

    def __init__(self, name=""):
        self.name = name
        self.wtok = None
        self.rtoks = {}
        self.dsem = None
        self.dcnt = 0


class Sched:
    ENG = ("pe", "act", "dve", "pool", "sp")

    def __init__(self, nc):
        self.nc = nc
        self.sem = {e: nc.alloc_semaphore(name="c_" + e) for e in self.ENG}
        self.cnt = {e: 0 for e in self.ENG}
        self.seen = {e: {} for e in self.ENG}
        self.ops = {e: [] for e in self.ENG}
        self.ndsem = 0
        self.dma_toks = {}

    def _need(self, eng, tok, waits):
        if tok is None:
            return
        sem, val = tok
        k = id(sem)
        if self.seen[eng].get(k, 0) >= val:
            return
        cur = waits.get(k)
        if cur is None or cur[1] < val:
            waits[k] = (sem, val)

    def _collect(self, eng, reads, writes, after=()):
        waits = {}
        for t in after:
            self._need(eng, t, waits)
        for b in reads:
            self._need(eng, b.wtok, waits)
        for b in writes:
            self._need(eng, b.wtok, waits)
            for e2, t in b.rtoks.items():
                if e2 != eng:
                    self._need(eng, t, waits)
        if eng == "pe":
            waits.pop(id(self.sem["pe"]), None)
        wl = list(waits.values())
        for sem, val in wl:
            self.seen[eng][id(sem)] = val
        return wl

    def op(self, eng, name, reads, writes, *args, after=(), **kw):
        wl = self._collect(eng, reads, writes, after)
        self.cnt[eng] += 1
        tok = (self.sem[eng], self.cnt[eng])
        self.ops[eng].append((wl, name, args, kw, (self.sem[eng], 1)))
        for b in reads:
            b.rtoks[eng] = tok
        for b in writes:
            b.wtok = tok
            b.rtoks = {}
        return tok

    def dma(self, q, name, reads, writes, *args, inc=16, sembuf=None, after=(), **kw):
        wl = self._collect(q, reads, writes, after)
        sb = sembuf or (writes[0] if writes else reads[0])
        if sb.dsem is None:
            sb.dsem = self.nc.alloc_semaphore(name="d%d" % self.ndsem)
            self.ndsem += 1
        sb.dcnt += inc
        tok = (sb.dsem, sb.dcnt)
        self.dma_toks[id(sb.dsem)] = tok
        self.ops[q].append((wl, name, args, kw, (sb.dsem, inc)))
        for b in reads:
            b.rtoks["dma"] = tok
        for b in writes:
            b.wtok = tok
            b.rtoks = {}
        return tok

    def wait_toks(self, eng, toks):
        waits = {}
        for t in toks:
            self._need(eng, t, waits)
        wl = list(waits.values())
        for sem, val in wl:
            self.seen[eng][id(sem)] = val
        if wl:
            self.ops[eng].append((wl, None, None, None, None))

    def barrier(self, engs=("pe", "act", "dve", "sp")):
        toks = [(self.sem[e], self.cnt[e]) for e in ("pe", "act", "dve") if self.cnt[e] > 0]
        toks += list(self.dma_toks.values())
        for e in engs:
            self.wait_toks(e, [t for t in toks if t[0] is not self.sem[e]])
        return toks

    def replay(self):
        nc = self.nc
        be = {"pe": "tensor", "act": "scalar", "dve": "vector", "pool": "gpsimd", "sp": "sync"}
        with nc.Block() as block:
            for ename in self.ENG:
                def body(e, ename=ename):
                    for wl, name, args, kw, inc in self.ops[ename]:
                        for sem, val in wl:
                            e.wait_ge(sem, val)
                        if name is not None:
                            ins = getattr(e, name)(*args, **kw)
                            ins.then_inc(inc[0], inc[1])
                getattr(block, be[ename])(body)


STOPS = ("l0f0", "l0mix", "l0f1", "l1f0", "l1mix", "l1f1", "final")
_DTB = {F32: 4, BF16: 2}


class Arena:
    def __init__(self, nc):
        self.nc = nc
        self.lo = (nc.sbuf_base + 63) // 64 * 64
        self.hi = nc.sbuf_top // 64 * 64
        self.n = 0

    def _bytes(self, shape, dtype):
        n = _DTB[dtype]
        for s in shape[1:]:
            n *= s
        return (n + 63) // 64 * 64

    def bot(self, name, shape, dtype):
        nb = self._bytes(shape, dtype)
        off = self.lo
        self.lo += nb
        assert self.lo <= self.hi, "SBUF arena overflow by %d at %s" % (self.lo - self.hi, name)
        self.n += 1
        self.last_off = off
        return self.nc.alloc_sbuf_tensor_at("%s_%d" % (name, self.n), shape, dtype, offset=off)

    def top(self, name, shape, dtype):
        nb = self._bytes(shape, dtype)
        self.hi -= nb
        assert self.lo <= self.hi, "SBUF arena overflow by %d at %s" % (self.lo - self.hi, name)
        self.n += 1
        return self.nc.alloc_sbuf_tensor_at("%s_%d" % (name, self.n), shape, dtype, offset=self.hi)


def build(ncores=8, stop="final"):
    nc = bass.Bass("TRN2", target_bir_lowering=False)
    JB = 288 // ncores
    SO = small_layout(JB)
    pairs = [[2 * i, 2 * i + 1] for i in range(ncores // 2)]
    allg = [list(range(ncores))]

    def din(name, shape, dtype=F32):
        return nc.dram_tensor(name, shape, dtype, kind="ExternalInput").ap()

    x_d = din("x", [T, D]); ctx_d = din("ctx", [TC, D])
    sm_d = din("smalls", [128, NSM]); rope_d = din("rope", [64, 2, T]); ident_d = din("ident", [128, 128])
    adaw_d = din("adaw", [D, JB * 128])
    w1_d = din("ffn_w1", [4, D, DFF]); w3_d = din("ffn_w3", [4, D, DFF]); w2_d = din("ffn_w2", [4, DFF, D])
    wdq_d = din("wdq", [D, 512]); wuq_d = din("wuq", [512, 4096]); wdkv_d = din("wdkv", [D, 640])
    wukv_d = din("wukv", [512, 4096]); wo_d = din("wo", [D, D])
    cw1_d = din("cw1", [D, 2 * D]); cw2_d = din("cw2", [D, D])
    out_d = nc.dram_tensor("out", [T, D], F32, kind="ExternalOutput").ap()
    ada_bi = nc.dram_tensor("ada_bi", [128, JB * 5], F32).ap()
    ada_bo = nc.dram_tensor("ada_bo", [ncores * 128, JB * 5], F32).ap()
    lat_bi = nc.dram_tensor("lat_bi", [640, TT], BF16).ap()
    lat_bo = nc.dram_tensor("lat_bo", [1280, TT], BF16).ap()
    hal_bi = nc.dram_tensor("hal_bi", [128, 512], BF16).ap()
    hal_bo = nc.dram_tensor("hal_bo", [256, 512], BF16).ap()

    S = Sched(nc)
    AR = Arena(nc)

    hT = AR.bot("hT", [128, KC, T], F32)
    ring = [AR.bot("ring%d" % i, [128, 4096], BF16) for i in range(6)]
    sm = AR.bot("sm", [128, NSM], F32)
    identf = AR.bot("identf", [128, 128], F32)
    identb = AR.bot("identb", [128, 128], BF16)
    onesb = AR.bot("onesb", [128, 128], BF16)
    epsc = AR.bot("epsc", [128, 1], F32)
    m_own = AR.bot("m_own", [128, 288], F32)
    m_ctx = AR.bot("m_ctx", [128, 288], F32)
    Aown = AR.bot("Aown", [128, 6, 16], F32)
    Cown = AR.bot("Cown", [128, 6, 16], F32)
    Actx = AR.bot("Actx", [128, 2, 16], F32)
    Cctx = AR.bot("Cctx", [128, 16], F32)
    gb2 = AR.bot("gb2", [128, 16], F32)
    G_LO = AR.lo
    G_HI = AR.hi
    ps = [nc.alloc_psum_tensor("ps%d" % i, [128, 512], F32) for i in range(8)]

    b_h = [[Buf("h%d_%d" % (c, g)) for g in range(2)] for c in range(KC)]
    b_hc = [Buf("hc%d" % c) for c in range(KC)]
    b_ps = [Buf("ps%d" % i) for i in range(8)]
    b_ring = [Buf("ring%d" % i) for i in range(6)]
    b_const = Buf("const")
    b_mods = Buf("mods")
    ring_state = {"i": 0, "busy": [False] * 6}

    def smv(name, idx=0, n=1):
        o = SO[name] + idx
        return sm[:, o:o + n]

    def ring_load(in_ap, k, n):
        s = ring_state["i"] % 6
        ring_state["i"] += 1
        assert not ring_state["busy"][s], "ring slot %d still has open consumers" % s
        ring_state["busy"][s] = True
        view = ring[s][:, 0:k * n].rearrange("p (k n) -> p k n", k=k)
        S.dma("pool", "dma_start", [], [b_ring[s]], out=view, in_=in_ap)
        return view, b_ring[s], s

    def ring_release(s):
        ring_state["busy"][s] = False

    def kblk(w2d, c0, ncol):
        return w2d[:, c0:c0 + ncol].rearrange("(k p) n -> p k n", p=128)

    evac_rr = {"i": 0}

    def evac_copy(out_ap, in_ap, reads, writes, eng=None):
        if eng is None:
            eng = ("act", "dve")[evac_rr["i"] % 2]
            evac_rr["i"] += 1
        if eng == "act":
            S.op("act", "activation", reads, writes, out=out_ap, in_=in_ap, func=AF.Copy)
        else:
            S.op("dve", "tensor_copy", reads, writes, out=out_ap, in_=in_ap)

    def ffn_load13(wi, sb, w13):
        for b in (2 * sb, 2 * sb + 1):
            w13[(b, 1)] = ring_load(kblk(w1_d[wi], b * 256, 256), 16, 256)
            w13[(b, 3)] = ring_load(kblk(w3_d[wi], b * 256, 256), 16, 256)

    def ffn_load2(wi, sb, w2):
        for half in range(2):
            r0 = sb * 512 + half * 256
            w2[(sb, half)] = ring_load(w2_d[wi][r0:r0 + 256, :].rearrange("(f p) n -> p f n", p=128), 2, D)

    S.dma("sp", "dma_start", [], [b_const], out=sm[:], in_=sm_d)
    S.dma("sp", "dma_start", [], [b_const], out=identf[:], in_=ident_d)
    S.op("dve", "tensor_copy", [b_const], [b_const], out=identb[:], in_=identf[:])
    S.op("dve", "memset", [], [b_const], onesb[:], 1.0)
    S.op("dve", "memset", [], [b_const], epsc[:], EPS)

    hcT = AR.bot("hcT", [128, KC, TC], F32)
    L0_LO = AR.lo
    mods_all = AR.bot("mods_all", [128, 288, 5], F32)
    scT = AR.bot("scT", [128, 16, 5], BF16)
    mpart = AR.bot("mpart", [128, JB, 5], F32)
    xs = AR.bot("xs", [128, 4, D], F32)
    b_xs = Buf("xs"); b_sc = Buf("sc"); b_mp = Buf("mpart")
    b_adabi = Buf("adabi"); b_adabo = Buf("adabo")

    S.op("act", "activation", [b_const], [b_sc], out=scT[:].rearrange("p k r -> p (k r)"),
         in_=smv("cond", 0, 80), func=AF.Silu)
    nslots = (JB + 1) // 2

    def ada_slots(lo, hi):
        for si in range(lo, min(hi, nslots)):
            nb = min(2, JB - 2 * si)
            wv, wb, ws = ring_load(kblk(adaw_d, si * 256, nb * 128), 16, nb * 128)
            for jj in range(nb):
                j = 2 * si + jj
                pb = j % 2
                for k in range(16):
                    S.op("pe", "matmul", [wb, b_sc], [b_ps[pb]], ps[pb][:, 0:5],
                         lhsT=wv[:, k, jj * 128:(jj + 1) * 128], rhs=scT[:, k, :], start=(k == 0), stop=(k == 15))
                S.op("dve", "tensor_scalar", [b_ps[pb], b_const], [b_mp], out=mpart[:, j, :], in0=ps[pb][:, 0:5],
                     scalar1=smv("adab", j), scalar2=None, op0=ALU.add)
            ring_release(ws)

    def x_group(tg):
        if tg < 2:
            S.dma("sp", "dma_start", [], [b_xs], out=xs[:],
                  in_=x_d[tg * 512:(tg + 1) * 512, :].rearrange("(t p) d -> p t d", p=128))
            nt = 4
        else:
            S.dma("sp", "dma_start", [], [b_xs], out=xs[:, 0, :], in_=ctx_d)
            nt = 1
        for c in range(KC):
            pb = 2 + c % 2
            for t in range(nt):
                S.op("pe", "transpose", [b_xs, b_const], [b_ps[pb]], out=ps[pb][:, t * 128:(t + 1) * 128],
                     in_=xs[:, t, c * 128:(c + 1) * 128], identity=identf[:])
            if tg < 2:
                evac_copy(hT[:, c, tg * 512:(tg + 1) * 512], ps[pb][:], [b_ps[pb]], [b_h[c][tg]])
            else:
                evac_copy(hcT[:, c, :], ps[pb][:, 0:128], [b_ps[pb]], [b_hc[c]])

    per = (nslots + 2) // 3
    for tg in range(3):
        ada_slots(tg * per, (tg + 1) * per)
        x_group(tg)
    S.dma("pool", "dma_start", [b_mp], [b_adabi], out=ada_bi, in_=mpart[:].rearrange("p j r -> p (j r)"))
    S.dma("pool", "collective_compute", [b_adabi], [b_adabo], "AllGather", ALU.bypass,
          replica_groups=allg, ins=[ada_bi], outs=[ada_bo], inc=1)
    S.dma("pool", "dma_start", [b_adabo], [b_mods],
          out=mods_all[:].rearrange("p (r j) q -> p r (j q)", r=ncores),
          in_=ada_bo.rearrange("(r p) f -> p r f", p=128))
    ffn0_pre = ({}, {})
    ffn_load13(0, 0, ffn0_pre[0])
    ffn_load2(0, 0, ffn0_pre[1])
    S.op("dve", "tensor_scalar", [b_mods, b_const], [b_mods], out=m_own[:], in0=mods_all[:, :, 0],
         scalar1=smv("sel", 0), scalar2=None, op0=ALU.mult)
    for r in range(1, 4):
        S.op("dve", "scalar_tensor_tensor", [b_mods, b_const], [b_mods], out=m_own[:], in0=mods_all[:, :, r],
             scalar=smv("sel", r), in1=m_own[:], op0=ALU.mult, op1=ALU.add)
    S.op("dve", "tensor_copy", [b_mods], [b_mods], out=m_ctx[:], in_=mods_all[:, :, 4])

    def G(l, mod):
        return l * 144 + mod * 16

    for l in range(2):
        for k in range(3):
            i6 = l * 3 + k
            S.op("dve", "scalar_tensor_tensor", [b_mods, b_const], [b_mods], out=Aown[:, i6, :],
                 in0=m_own[:, G(l, 3 * k + 1):G(l, 3 * k + 1) + 16], scalar=1.0,
                 in1=sm[:, SO["normg"] + i6 * 16:SO["normg"] + i6 * 16 + 16], op0=ALU.add, op1=ALU.mult)
            S.op("dve", "tensor_scalar", [b_mods], [b_mods], out=Cown[:, i6, :],
                 in0=m_own[:, G(l, 3 * k + 2):G(l, 3 * k + 2) + 16],
                 scalar1=(1.0 if k == 1 else 0.5), scalar2=None, op0=ALU.mult)
    for k in range(2):
        S.op("dve", "scalar_tensor_tensor", [b_mods, b_const], [b_mods], out=Actx[:, k, :],
             in0=m_ctx[:, G(0, 3 * k + 1):G(0, 3 * k + 1) + 16], scalar=1.0,
             in1=sm[:, SO["normg"] + k * 16:SO["normg"] + k * 16 + 16], op0=ALU.add, op1=ALU.mult)
    S.op("dve", "tensor_scalar", [b_mods], [b_mods], out=Cctx[:], in0=m_ctx[:, G(0, 2):G(0, 2) + 16],
         scalar1=0.5, scalar2=None, op0=ALU.mult)
    S.op("dve", "tensor_tensor", [b_mods, b_const], [b_mods], out=gb2[:], in0=Cown[:, 4, :],
         in1=sm[:, SO["b2"]:SO["b2"] + 16], op=ALU.mult)

    def modAP(l, k, kind, ctx=False):
        i6 = l * 3 + k
        if kind == "A":
            return (lambda c: Actx[:, k, c:c + 1]) if ctx else (lambda c: Aown[:, i6, c:c + 1])
        if kind == "B":
            g0 = G(l, 3 * k)
            return (lambda c: m_ctx[:, g0 + c:g0 + c + 1]) if ctx else (lambda c: m_own[:, g0 + c:g0 + c + 1])
        return (lambda c: Cctx[:, c:c + 1]) if ctx else (lambda c: Cown[:, i6, c:c + 1])

    bar = S.barrier()
    AR.lo = L0_LO

    class Grp:
        def __init__(self, gi, u0, n, ctx):
            self.gi, self.u0, self.n, self.ctx = gi, u0, n, ctx

        def h(self, c):
            return hcT[:, c, :] if self.ctx else hT[:, c, self.u0:self.u0 + self.n]

        def hb(self, c):
            return b_hc[c] if self.ctx else b_h[c][self.gi]

    g_lat = [Grp(0, 0, 512, False), Grp(1, 512, 512, False)]
    g_ctx = Grp(2, 1024, 128, True)

    def make_scr(tag):
        scr = {}
        scr["sq"] = [AR.bot(tag + "sq%d" % i, [128, 512], BF16) for i in range(2)]
        scr["b_sq"] = [Buf() for _ in range(2)]
        scr["tm"] = [AR.bot(tag + "tm%d" % i, [128, 512], F32) for i in range(2)]
        scr["b_tm"] = [Buf() for _ in range(2)]
        scr["rs"] = AR.bot(tag + "rs", [128, 512], F32)
        scr["b_rs"] = Buf()
        return scr

    def norm_mod(groups, l, k, uT, b_u, scr, plain_g=None, out_f32=None, ucol=None):
        for g in groups:
            n = g.n
            u0 = g.u0 if ucol is None else ucol
            bu = None if b_u is None else (b_u[g.gi] if ucol is None else b_u[0])
            st = 6 + (g.gi % 2)
            for c in range(KC):
                q = c % 2
                if c % 2 == 0:
                    S.op("act", "activation", [g.hb(c)], [scr["b_sq"][q]], out=scr["sq"][q][:, 0:n], in_=g.h(c),
                         func=AF.Square)
                else:
                    S.op("dve", "tensor_tensor", [g.hb(c)], [scr["b_sq"][q]], out=scr["sq"][q][:, 0:n], in0=g.h(c),
                         in1=g.h(c), op=ALU.mult)
                S.op("pe", "matmul", [scr["b_sq"][q], b_const], [b_ps[st]], ps[st][:, 0:n], lhsT=onesb[:],
                     rhs=scr["sq"][q][:, 0:n], start=(c == 0), stop=(c == KC - 1))
            S.op("act", "activation", [b_ps[st], b_const], [scr["b_rs"]], out=scr["rs"][:, 0:n], in_=ps[st][:, 0:n],
                 func=AF.Sqrt, bias=epsc[:, 0:1], scale=1.0 / D)
            S.op("dve", "reciprocal", [scr["b_rs"]], [scr["b_rs"]], out=scr["rs"][:, 0:n], in_=scr["rs"][:, 0:n])
            for c in range(KC):
                q = c % 2
                if plain_g is not None:
                    S.op("dve", "scalar_tensor_tensor", [g.hb(c), scr["b_rs"], b_const], [out_f32[1][c]],
                         out=out_f32[0][:, c, 0:n], in0=g.h(c), scalar=plain_g(c), in1=scr["rs"][:, 0:n],
                         op0=ALU.mult, op1=ALU.mult)
                    continue
                fa = modAP(l, k, "A", g.ctx); fb = modAP(l, k, "B", g.ctx)
                S.op("dve", "scalar_tensor_tensor", [g.hb(c), scr["b_rs"], b_mods], [scr["b_tm"][q]],
                     out=scr["tm"][q][:, 0:n], in0=g.h(c), scalar=fa(c), in1=scr["rs"][:, 0:n],
                     op0=ALU.mult, op1=ALU.mult)
                S.op("act", "activation", [scr["b_tm"][q], b_mods], [bu], out=uT[:, c, u0:u0 + n],
                     in_=scr["tm"][q][:, 0:n], func=AF.Identity, bias=fb(c), scale=1.0)

    def ffn(wi, l, k, groups, uT, b_u, pre=None):
        gbuf = [AR.bot("g%d" % i, [128, 4, TT if len(groups) == 3 else T], BF16) for i in range(2)]
        sil = [AR.bot("sil%d" % i, [128, 512], BF16) for i in range(2)]
        b_g = [[[Buf() for _ in range(3)] for _ in range(4)] for _ in range(2)]
        b_sil = [Buf(), Buf()]
        st = {"pp": 0, "si": 0, "pc": 0}
        w13, w2 = pre if pre is not None else ({}, {})

        def GU(sb):
            par = sb % 2
            for bb in range(2):
                b = 2 * sb + bb
                v1, bw1, s1 = w13[(b, 1)]
                v3, bw3, s3 = w13[(b, 3)]
                for jj in range(2):
                    j4 = bb * 2 + jj
                    for g in groups:
                        n = g.n
                        pa, pb = ((0, 1), (2, 3))[st["pp"]]
                        st["pp"] ^= 1
                        for kk in range(KC):
                            S.op("pe", "matmul", [bw1, b_u[g.gi]], [b_ps[pa]], ps[pa][:, 0:n],
                                 lhsT=v1[:, kk, jj * 128:(jj + 1) * 128], rhs=uT[:, kk, g.u0:g.u0 + n],
                                 start=(kk == 0), stop=(kk == KC - 1))
                        for kk in range(KC):
                            S.op("pe", "matmul", [bw3, b_u[g.gi]], [b_ps[pb]], ps[pb][:, 0:n],
                                 lhsT=v3[:, kk, jj * 128:(jj + 1) * 128], rhs=uT[:, kk, g.u0:g.u0 + n],
                                 start=(kk == 0), stop=(kk == KC - 1))
                        si = st["si"]
                        st["si"] ^= 1
                        S.op("act", "activation", [b_ps[pa]], [b_sil[si]], out=sil[si][:, 0:n], in_=ps[pa][:, 0:n],
                             func=AF.Silu)
                        S.op("dve", "tensor_tensor", [b_ps[pb], b_sil[si]], [b_g[par][j4][g.gi]],
                             out=gbuf[par][:, j4, g.u0:g.u0 + n], in0=ps[pb][:, 0:n], in1=sil[si][:, 0:n],
                             op=ALU.mult)
                ring_release(s1)
                ring_release(s3)

        def W2(sb):
            par = sb % 2
            for c in range(KC):
                for g in groups:
                    n = g.n
                    pc = 4 + st["pc"]
                    st["pc"] ^= 1
                    for j4 in range(4):
                        v2, bw2, s2 = w2[(sb, j4 // 2)]
                        S.op("pe", "matmul", [bw2, b_g[par][j4][g.gi]], [b_ps[pc]], ps[pc][:, 0:n],
                             lhsT=v2[:, j4 % 2, c * 128:(c + 1) * 128], rhs=gbuf[par][:, j4, g.u0:g.u0 + n],
                             start=(j4 == 0), stop=(j4 == 3))
                    fc = modAP(l, k, "C", g.ctx)
                    S.op("dve", "scalar_tensor_tensor", [b_ps[pc], g.hb(c), b_mods], [g.hb(c)], out=g.h(c),
                         in0=ps[pc][:, 0:n], scalar=fc(c), in1=g.h(c), op0=ALU.mult, op1=ALU.add)
            for half in range(2):
                ring_release(w2[(sb, half)][2])

        NSB = DFF // 512
        for sb in range(NSB):
            if not (sb == 0 and pre is not None):
                ffn_load13(wi, sb, w13)
                ffn_load2(wi, sb, w2)
            GU(sb)
            W2(sb)

    def final(do_norm):
        AR.lo, AR.hi = G_LO, G_HI
        y = AR.bot("y", [128, KC, 512], F32)
        b_y = [Buf() for _ in range(KC)]
        ot = [AR.bot("ot%d" % i, [128, D], F32) for i in range(2)]
        b_ot = [Buf(), Buf()]
        scr = make_scr("fin")
        toks = []
        for g in g_lat:
            if do_norm:
                norm_mod([g], 0, 0, None, None, scr, plain_g=lambda c: smv("finalg", c), out_f32=(y, b_y))
            for t in range(4):
                oi = t % 2
                for qd in range(4):
                    pb = qd % 4
                    for cc in range(4):
                        c = qd * 4 + cc
                        if do_norm:
                            S.op("pe", "transpose", [b_y[c], b_const], [b_ps[pb]], out=ps[pb][:, cc * 128:(cc + 1) * 128],
                                 in_=y[:, c, t * 128:(t + 1) * 128], identity=identf[:])
                        else:
                            S.op("pe", "transpose", [b_h[c][g.gi], b_const], [b_ps[pb]],
                                 out=ps[pb][:, cc * 128:(cc + 1) * 128],
                                 in_=hT[:, c, g.u0 + t * 128:g.u0 + (t + 1) * 128], identity=identf[:])
                    evac_copy(ot[oi][:, qd * 512:(qd + 1) * 512], ps[pb][:], [b_ps[pb]], [b_ot[oi]])
                r0 = g.u0 + t * 128
                toks.append(S.dma("sp", "dma_start", [b_ot[oi]], [], out=out_d[r0:r0 + 128, :], in_=ot[oi][:]))
        S.wait_toks("sp", toks)

    def finish():
        S.replay()
        return nc

    uT = AR.bot("uT", [128, KC, TT], BF16)
    b_u = [Buf("u0"), Buf("u1"), Buf("u2")]
    scr = make_scr("n0")
    norm_mod(g_lat + [g_ctx], 0, 0, uT, b_u, scr)
    ffn(0, 0, 0, g_lat + [g_ctx], uT, b_u, pre=ffn0_pre)
    bar = S.barrier()
    AR.lo = L0_LO
    if stop == "l0f0":
        final(False)
        return finish()

    cq = AR.top("cq", [128, 4, T], BF16)
    ropeT = AR.top("ropeT", [64, 2, T], F32)
    b_rope = Buf("rope")
    S.dma("sp", "dma_start", [], [b_rope], out=ropeT[:], in_=rope_d)
    b_cq = [Buf(), Buf()]
    MIX_HI = AR.hi
    uMs = [AR.bot("uM%d" % i, [128, KC, 512], BF16) for i in range(2)]
    b_uMs = [[Buf("uM0")], [Buf("uM1")]]
    scr = make_scr("nM")
    lat_own = AR.bot("lat_own", [128, 5, TT], BF16)
    b_lat_own = [Buf() for _ in range(3)]
    raw = AR.bot("raw", [128, 4, 512], F32)
    b_raw = [Buf() for _ in range(4)]
    t1 = AR.bot("t1", [64, 512], F32); t2 = AR.bot("t2", [64, 512], F32)
    b_t1 = Buf(); b_t2 = Buf()

    wkv = [ring_load(kblk(wdkv_d, 0, 256), 16, 256), ring_load(kblk(wdkv_d, 256, 256), 16, 256),
           ring_load(kblk(wdkv_d, 512, 128), 16, 128)]
    wq = [ring_load(kblk(wdq_d, 0, 256), 16, 256), ring_load(kblk(wdq_d, 256, 256), 16, 256)]

    def lowrank_norm(wslots, g, gname, dst, dst_b, uM, b_uM):
        n = g.n
        st = 6 + (g.gi % 2)
        for m in range(4):
            wv, wb, _ = wslots[m // 2]
            pb = m % 2
            for kk in range(KC):
                S.op("pe", "matmul", [wb, b_uM[0]], [b_ps[pb]], ps[pb][:, 0:n],
                     lhsT=wv[:, kk, (m % 2) * 128:(m % 2 + 1) * 128], rhs=uM[:, kk, 0:n],
                     start=(kk == 0), stop=(kk == KC - 1))
            S.op("act", "activation", [b_ps[pb]], [b_raw[m]], out=raw[:, m, 0:n], in_=ps[pb][:, 0:n], func=AF.Copy)
            q = m % 2
            S.op("dve", "tensor_tensor", [b_raw[m]], [scr["b_sq"][q]], out=scr["sq"][q][:, 0:n], in0=raw[:, m, 0:n],
                 in1=raw[:, m, 0:n], op=ALU.mult)
            S.op("pe", "matmul", [scr["b_sq"][q], b_const], [b_ps[st]], ps[st][:, 0:n], lhsT=onesb[:],
                 rhs=scr["sq"][q][:, 0:n], start=(m == 0), stop=(m == 3))
        S.op("act", "activation", [b_ps[st], b_const], [scr["b_rs"]], out=scr["rs"][:, 0:n], in_=ps[st][:, 0:n],
             func=AF.Sqrt, bias=epsc[:, 0:1], scale=1.0 / 512)
        S.op("dve", "reciprocal", [scr["b_rs"]], [scr["b_rs"]], out=scr["rs"][:, 0:n], in_=scr["rs"][:, 0:n])
        for m in range(4):
            S.op("dve", "scalar_tensor_tensor", [b_raw[m], scr["b_rs"], b_const], [dst_b], out=dst[:, m, g.u0:g.u0 + n],
                 in0=raw[:, m, 0:n], scalar=smv(gname, m), in1=scr["rs"][:, 0:n], op0=ALU.mult, op1=ALU.mult)

    def rope_pair(p_bank, s_bank, n, tok0, out_ap, out_b):
        S.op("dve", "tensor_tensor", [b_ps[p_bank], b_rope], [b_t1], out=t1[:, 0:n], in0=ps[p_bank][0:64, 0:n],
             in1=ropeT[:, 0, tok0:tok0 + n], op=ALU.mult)
        S.op("dve", "tensor_tensor", [b_ps[s_bank], b_rope], [b_t2], out=t2[:, 0:n], in0=ps[s_bank][0:64, 0:n],
             in1=ropeT[:, 1, tok0:tok0 + n], op=ALU.mult)
        S.op("dve", "tensor_tensor", [b_t1, b_t2], [out_b], out=out_ap, in0=t1[:, 0:n], in1=t2[:, 0:n], op=ALU.add)

    def lat_group(g, uM, b_uM):
        n = g.n
        lowrank_norm(wkv, g, "gkv", lat_own, b_lat_own[g.gi], uM, b_uM)
        wv, wb, _ = wkv[2]
        for hb_, bank in ((0, 2), (1, 3)):
            if g.ctx and hb_ == 1:
                continue
            for kk in range(KC):
                S.op("pe", "matmul", [wb, b_uM[0]], [b_ps[bank]], ps[bank][0:64, 0:n],
                     lhsT=wv[:, kk, hb_ * 64:(hb_ + 1) * 64], rhs=uM[:, kk, 0:n],
                     start=(kk == 0), stop=(kk == KC - 1))
        if g.ctx:
            S.op("act", "activation", [b_ps[2]], [b_lat_own[g.gi]], out=lat_own[0:64, 4, g.u0:g.u0 + n],
                 in_=ps[2][0:64, 0:n], func=AF.Copy)
        else:
            rope_pair(2, 3, n, g.u0, lat_own[0:64, 4, g.u0:g.u0 + n], b_lat_own[g.gi])
            lowrank_norm(wq, g, "gq", cq, b_cq[g.gi], uM, b_uM)

    gl = g_lat + [g_ctx]
    norm_mod([gl[0]], 0, 1, uMs[0], b_uMs[0], scr, ucol=0)
    norm_mod([gl[1]], 0, 1, uMs[1], b_uMs[1], scr, ucol=0)
    lat_group(gl[0], uMs[0], b_uMs[0])
    norm_mod([gl[2]], 0, 1, uMs[0], b_uMs[0], scr, ucol=0)
    lat_group(gl[1], uMs[1], b_uMs[1])
    lat_group(gl[2], uMs[0], b_uMs[0])
    for w in wkv + wq:
        ring_release(w[2])

    b_latbi = Buf("latbi"); b_latbo = Buf("latbo")
    S.op("dve", "memset", [], list(b_lat_own), lat_own[64:128, 4, :], 0.0)
    S.dma("pool", "dma_start", list(b_lat_own), [b_latbi], out=lat_bi.rearrange("(k p) t -> p k t", p=128),
          in_=lat_own[:])
    S.dma("pool", "collective_compute", [b_latbi], [b_latbo], "AllGather", ALU.bypass,
          replica_groups=pairs, ins=[lat_bi], outs=[lat_bo], inc=1)
    bar = S.barrier()
    AR.lo = G_LO
    lat_all = AR.top("lat_all", [128, 5, 2 * TT], BF16)
    b_lat_all = Buf("lat_all")
    for r in range(2):
        S.dma("pool", "dma_start", [b_latbo], [b_lat_all], out=lat_all[:, :, r * TT:(r + 1) * TT],
              in_=lat_bo[r * 640:(r + 1) * 640, :].rearrange("(k p) t -> p k t", p=128), after=bar)
    NK = 2 * TT
    NKC = NK // 128

    qn = AR.bot("qn", [128, T], BF16)
    qpe = AR.bot("qpe", [64, T], BF16)
    kn = AR.bot("kn", [128, NK], BF16)
    Vg = AR.bot("Vg", [128, NKC, 512], BF16)
    oT = AR.bot("oT", [128, 4, T], BF16)
    ex = [AR.bot("ex%d" % i, [128, 512], BF16) for i in range(4)]
    rec = AR.bot("rec", [128, 512], F32)
    o32 = AR.bot("o32", [128, 512], F32)
    b_o32 = Buf()
    deferred = []
    t1 = AR.bot("t1", [64, 512], F32); t2 = AR.bot("t2", [64, 512], F32)
    b_t1 = Buf(); b_t2 = Buf()
    b_qn = [Buf(), Buf()]
    b_qpe = [Buf(), Buf()]
    b_kn = [Buf() for _ in range(5)]
    b_V = [Buf() for _ in range(NKC)]
    b_oT = [[Buf(), Buf()] for _ in range(4)]
    b_ex = [Buf() for _ in range(4)]
    b_rec = Buf()
    exi = {"i": 0, "st": 0}
    gate_mix0 = modAP(0, 1, "C")

    for hg in range(4):
        wqv, wqb, wqs = ring_load(kblk(wuq_d, hg * 1024, 1024), 4, 1024)
        wkvv, wkvb, wkvs = ring_load(kblk(wukv_d, hg * 1024, 1024), 4, 1024)
        wo_s = [ring_load(wo_d[hg * 512:(hg + 1) * 512, hf * 1024:(hf + 1) * 1024].rearrange("(k p) n -> p k n", p=128),
                          4, 1024) for hf in range(2)]
        for kc in range(NKC):
            pb = kc % 2
            for kk in range(4):
                S.op("pe", "matmul", [b_lat_all, wkvb], [b_ps[pb]], ps[pb][:],
                     lhsT=lat_all[:, kk, kc * 128:(kc + 1) * 128], rhs=wkvv[:, kk, 512:1024],
                     start=(kk == 0), stop=(kk == 3))
            evac_copy(Vg[:, kc, :], ps[pb][:], [b_ps[pb]], [b_V[kc]])
        for hh in range(4):
            for tg in range(2):
                c0 = tg * 512
                pb = tg % 2
                for kk in range(4):
                    S.op("pe", "matmul", [wqb, b_cq[tg]], [b_ps[pb]], ps[pb][:],
                         lhsT=wqv[:, kk, hh * 256:hh * 256 + 128], rhs=cq[:, kk, c0:c0 + 512],
                         start=(kk == 0), stop=(kk == 3))
                evac_copy(qn[:, c0:c0 + 512], ps[pb][:], [b_ps[pb]], [b_qn[tg]], eng="act")
                for hb_, bank in ((0, 2), (1, 3)):
                    for kk in range(4):
                        S.op("pe", "matmul", [wqb, b_cq[tg]], [b_ps[bank]], ps[bank][0:64, :],
                             lhsT=wqv[:, kk, hh * 256 + 128 + hb_ * 64:hh * 256 + 192 + hb_ * 64],
                             rhs=cq[:, kk, c0:c0 + 512], start=(kk == 0), stop=(kk == 3))
                rope_pair(2, 3, 512, c0, qpe[:, c0:c0 + 512], b_qpe[tg])
            for ng in range(5):
                k0 = ng * 512
                n = min(512, NK - k0)
                pb = ng % 2
                for kk in range(4):
                    S.op("pe", "matmul", [wkvb, b_lat_all], [b_ps[pb]], ps[pb][:, 0:n],
                         lhsT=wkvv[:, kk, hh * 128:(hh + 1) * 128], rhs=lat_all[:, kk, k0:k0 + n],
                         start=(kk == 0), stop=(kk == 3))
                evac_copy(kn[:, k0:k0 + n], ps[pb][:, 0:n], [b_ps[pb]], [b_kn[ng]])
            for tg in range(2):
                c0 = tg * 512

                def ST(kc, tg=tg, c0=c0):
                    sb_ = 4 + (exi["st"] % 2)
                    exi["st"] += 1
                    S.op("pe", "matmul", [b_kn[kc // 4], b_qn[tg]], [b_ps[sb_]], ps[sb_][:],
                         lhsT=kn[:, kc * 128:(kc + 1) * 128], rhs=qn[:, c0:c0 + 512], start=True, stop=False)
                    S.op("pe", "matmul", [b_lat_all, b_qpe[tg]], [b_ps[sb_]], ps[sb_][:],
                         lhsT=lat_all[0:64, 4, kc * 128:(kc + 1) * 128], rhs=qpe[:, c0:c0 + 512],
                         start=False, stop=True)
                    e = exi["i"] % 4
                    exi["i"] += 1
                    S.op("act", "activation", [b_ps[sb_]], [b_ex[e]], out=ex[e][:], in_=ps[sb_][:], func=AF.Exp,
                         scale=ATTN_SCALE)
                    return e

                def PV(kc, e, hh=hh):
                    S.op("pe", "matmul", [b_V[kc], b_ex[e]], [b_ps[6]], ps[6][:],
                         lhsT=Vg[:, kc, hh * 128:(hh + 1) * 128], rhs=ex[e][:], start=(kc == 0), stop=(kc == NKC - 1))
                    S.op("pe", "matmul", [b_const, b_ex[e]], [b_ps[7]], ps[7][:],
                         lhsT=onesb[:], rhs=ex[e][:], start=(kc == 0), stop=(kc == NKC - 1))
                pend = [ST(0)]
                for kc in range(NKC):
                    if kc + 1 < NKC:
                        pend.append(ST(kc + 1))
                    PV(kc, pend.pop(0))
                    if kc == 3 and deferred:
                        deferred.pop(0)()
                S.op("dve", "tensor_copy", [b_ps[7]], [b_rec], out=rec[:], in_=ps[7][:])
                S.op("dve", "tensor_copy", [b_ps[6]], [b_o32], out=o32[:], in_=ps[6][:])

                def norm_o(hh=hh, tg=tg, c0=c0):
                    S.op("dve", "reciprocal", [b_rec], [b_rec], out=rec[:], in_=rec[:])
                    S.op("dve", "tensor_tensor", [b_o32, b_rec], [b_oT[hh][tg]], out=oT[:, hh, c0:c0 + 512],
                         in0=o32[:], in1=rec[:], op=ALU.mult)
                deferred.append(norm_o)
        while deferred:
            deferred.pop(0)()
        for tg in range(2):
            for c in range(KC):
                wv, wb, _ = wo_s[c // 8]
                pb = c % 2
                for hh in range(4):
                    S.op("pe", "matmul", [wb, b_oT[hh][tg]], [b_ps[pb]], ps[pb][:],
                         lhsT=wv[:, hh, (c % 8) * 128:(c % 8 + 1) * 128], rhs=oT[:, hh, tg * 512:(tg + 1) * 512],
                         start=(hh == 0), stop=(hh == 3))
                S.op("dve", "scalar_tensor_tensor", [b_ps[pb], b_h[c][tg], b_mods], [b_h[c][tg]],
                     out=hT[:, c, tg * 512:(tg + 1) * 512], in0=ps[pb][:], scalar=gate_mix0(c),
                     in1=hT[:, c, tg * 512:(tg + 1) * 512], op0=ALU.mult, op1=ALU.add)
        ring_release(wqs); ring_release(wkvs); ring_release(wo_s[0][2]); ring_release(wo_s[1][2])
    bar = S.barrier()
    AR.lo, AR.hi = G_LO, G_HI
    if stop == "l0mix":
        final(False)
        return finish()

    def lat_ffn(wi, l, k):
        AR.lo, AR.hi = G_LO, G_HI
        u = AR.bot("uL", [128, KC, T], BF16)
        bu = [Buf(), Buf()]
        sc = make_scr("nL")
        norm_mod(g_lat, l, k, u, bu, sc)
        ffn(wi, l, k, g_lat, u, bu)
        b = S.barrier()
        AR.lo, AR.hi = G_LO, G_HI
        return b

    bar = lat_ffn(1, 0, 2)
    if stop == "l0f1":
        final(False)
        return finish()
    bar = lat_ffn(2, 1, 0)
    if stop == "l1f0":
        final(False)
        return finish()

    GL = T + 2 * HALO
    glu = AR.bot("glu", [128, KC, GL], BF16)
    b_glu = [[Buf(), Buf()] for _ in range(KC)]
    b_halo = Buf("halo")
    hal_r = AR.bot("hal_r", [128, 2, KC, 32], BF16)
    dg = [AR.bot("dg%d" % i, [128, 128], BF16) for i in range(8)]
    b_dg = [Buf() for _ in range(8)]
    b_hs = Buf(); b_hr = Buf()
    C_LO = AR.lo
    hal_s = AR.bot("hal_s", [128, KC, 32], BF16)
    sig = [AR.bot("sig%d" % i, [128, 512], F32) for i in range(2)]
    b_sig = [Buf(), Buf()]
    uC = AR.bot("uC", [128, KC, T], BF16)
    buC = [Buf(), Buf()]
    sc = make_scr("nC")
    norm_mod(g_lat, 1, 1, uC, buC, sc)
    for c in range(KC):
        wa = ring_load(kblk(cw1_d, c * 128, 128), 16, 128)
        wg = ring_load(kblk(cw1_d, D + c * 128, 128), 16, 128)
        for tg in range(2):
            c0 = tg * 512
            pa, pb = ((0, 1), (2, 3))[tg]
            for kk in range(KC):
                S.op("pe", "matmul", [wa[1], buC[tg]], [b_ps[pa]], ps[pa][:], lhsT=wa[0][:, kk, :],
                     rhs=uC[:, kk, c0:c0 + 512], start=(kk == 0), stop=(kk == KC - 1))
            for kk in range(KC):
                S.op("pe", "matmul", [wg[1], buC[tg]], [b_ps[pb]], ps[pb][:], lhsT=wg[0][:, kk, :],
                     rhs=uC[:, kk, c0:c0 + 512], start=(kk == 0), stop=(kk == KC - 1))
            S.op("act", "activation", [b_ps[pb], b_const], [b_sig[tg]], out=sig[tg][:], in_=ps[pb][:], func=AF.Sigmoid,
                 bias=smv("b1", 16 + c), scale=1.0)
            S.op("dve", "scalar_tensor_tensor", [b_ps[pa], b_sig[tg], b_const], [b_glu[c][tg]],
                 out=glu[:, c, HALO + c0:HALO + c0 + 512], in0=ps[pa][:], scalar=smv("b1", c), in1=sig[tg][:],
                 op0=ALU.add, op1=ALU.mult)
        ring_release(wa[2]); ring_release(wg[2])
    allglu = [b for cc in b_glu for b in cc]
    S.op("dve", "tensor_copy", allglu, [b_hs], out=hal_s[:, :, 0:16], in_=glu[:, :, HALO:HALO + 16])
    S.op("dve", "tensor_copy", allglu, [b_hs], out=hal_s[:, :, 16:32], in_=glu[:, :, T:T + 16])
    b_halbi = Buf(); b_halbo = Buf()
    S.dma("pool", "dma_start", [b_hs], [b_halbi], out=hal_bi, in_=hal_s[:].rearrange("p c t -> p (c t)"))
    S.dma("pool", "collective_compute", [b_halbi], [b_halbo], "AllGather", ALU.bypass,
          replica_groups=pairs, ins=[hal_bi], outs=[hal_bo], inc=1)
    S.dma("pool", "dma_start", [b_halbo], [b_hr], out=hal_r[:].rearrange("p r c t -> p r (c t)"),
          in_=hal_bo.rearrange("(r p) f -> p r f", p=128), after=bar)
    S.op("dve", "tensor_scalar", [b_hr, b_const], [b_halo], out=glu[:, :, 0:HALO], in0=hal_r[:, 0, :, 16:32],
         scalar1=smv("sel", 5), scalar2=None, op0=ALU.mult)
    S.op("dve", "scalar_tensor_tensor", [b_hr, b_const, b_halo], [b_halo], out=glu[:, :, 0:HALO],
         in0=hal_r[:, 1, :, 16:32], scalar=smv("sel", 6), in1=glu[:, :, 0:HALO], op0=ALU.mult, op1=ALU.add)
    S.op("dve", "tensor_scalar", [b_hr, b_const], [b_halo], out=glu[:, :, HALO + T:GL], in0=hal_r[:, 0, :, 0:16],
         scalar1=smv("sel", 7), scalar2=None, op0=ALU.mult)
    S.op("dve", "scalar_tensor_tensor", [b_hr, b_const, b_halo], [b_halo], out=glu[:, :, HALO + T:GL],
         in0=hal_r[:, 1, :, 0:16], scalar=smv("sel", 8), in1=glu[:, :, HALO + T:GL], op0=ALU.mult, op1=ALU.add)
    bar = S.barrier()
    AR.lo = C_LO

    v = AR.bot("v", [128, KC, 512], F32)
    z_al = nc.alloc_sbuf_tensor_at("z_alias", [128, KC, 1024], BF16, offset=AR.last_off)
    zb = [z_al[:, c, 0:512] for c in range(KC)]
    vb = AR.bot("vb", [128, 512], BF16)
    vq = AR.bot("vq", [128, 512], BF16)
    mean = AR.bot("mean", [128, 512], F32); rstd = AR.bot("rstd", [128, 512], F32)
    tt = [AR.bot("tt%d" % i, [128, 512], F32) for i in range(2)]
    b_v = [Buf() for _ in range(KC)]
    b_vb = Buf(); b_vq = Buf(); b_mean = Buf(); b_rstd = Buf()
    b_tt = [Buf(), Buf()]
    gate_mix1 = modAP(1, 1, "C")
    dgi = {"i": 0}
    for tg in range(2):
        c0 = tg * 512
        taps = (list(range(15, CW)) + list(range(0, 15))) if tg == 0 else list(range(CW))
        for c in range(KC):
            pc = 4 + (c % 2)
            for ti, kt in enumerate(taps):
                di = dgi["i"] % 8
                dgi["i"] += 1
                S.op("dve", "tensor_scalar", [b_const], [b_dg[di]], out=dg[di][:], in0=identb[:],
                     scalar1=smv("dw", c * CW + kt), scalar2=None, op0=ALU.mult)
                touches = (kt < 15) if tg == 0 else (kt > 15)
                rd = [b_dg[di], b_glu[c][0], b_glu[c][1]] + ([b_halo] if touches else [])
                S.op("pe", "matmul", rd, [b_ps[pc]], ps[pc][:], lhsT=dg[di][:],
                     rhs=glu[:, c, c0 + kt + 1:c0 + kt + 1 + 512], start=(ti == 0), stop=(ti == CW - 1))
            S.op("act", "activation", [b_ps[pc], b_const], [b_v[c]], out=v[:, c, :], in_=ps[pc][:], func=AF.Identity,
                 bias=smv("dwb", c), scale=1.0)
            S.op("dve", "tensor_copy", [b_v[c]], [b_vb], out=vb[:], in_=v[:, c, :])
            S.op("dve", "tensor_tensor", [b_v[c]], [b_vq], out=vq[:], in0=v[:, c, :], in1=v[:, c, :], op=ALU.mult)
            S.op("pe", "matmul", [b_vb, b_const], [b_ps[6]], ps[6][:], lhsT=onesb[:], rhs=vb[:],
                 start=(c == 0), stop=(c == KC - 1))
            S.op("pe", "matmul", [b_vq, b_const], [b_ps[7]], ps[7][:], lhsT=onesb[:], rhs=vq[:],
                 start=(c == 0), stop=(c == KC - 1))
        S.op("act", "activation", [b_ps[6]], [b_mean], out=mean[:], in_=ps[6][:], func=AF.Identity, scale=1.0 / D)
        S.op("dve", "tensor_tensor", [b_mean], [b_tt[0]], out=tt[0][:], in0=mean[:], in1=mean[:], op=ALU.mult)
        S.op("dve", "scalar_tensor_tensor", [b_ps[7], b_tt[0]], [b_rstd], out=rstd[:], in0=ps[7][:], scalar=1.0 / D,
             in1=tt[0][:], op0=ALU.mult, op1=ALU.subtract)
        S.op("act", "activation", [b_rstd, b_const], [b_rstd], out=rstd[:], in_=rstd[:], func=AF.Sqrt, bias=epsc[:, 0:1],
             scale=1.0)
        S.op("dve", "reciprocal", [b_rstd], [b_rstd], out=rstd[:], in_=rstd[:])
        for c in range(KC):
            q = c % 2
            S.op("dve", "tensor_tensor", [b_v[c], b_mean], [b_tt[q]], out=tt[q][:], in0=v[:, c, :], in1=mean[:],
                 op=ALU.subtract)
            S.op("dve", "tensor_tensor", [b_tt[q], b_rstd], [b_tt[q]], out=tt[q][:], in0=tt[q][:], in1=rstd[:],
                 op=ALU.mult)
            S.op("act", "activation", [b_tt[q], b_const], [b_v[c]], out=zb[c], in_=tt[q][:], func=AF.Silu,
                 bias=smv("lnb", c), scale=smv("lng", c))
        for j in range(8):
            wv, wb, ws = ring_load(kblk(cw2_d, j * 256, 256), 16, 256)
            for jj in range(2):
                c2 = 2 * j + jj
                pb = c2 % 2
                for kk in range(KC):
                    S.op("pe", "matmul", [wb, b_v[kk]], [b_ps[pb]], ps[pb][:],
                         lhsT=wv[:, kk, jj * 128:(jj + 1) * 128], rhs=zb[kk],
                         start=(kk == 0), stop=(kk == KC - 1))
                q = c2 % 2
                S.op("act", "activation", [b_ps[pb], b_mods], [b_tt[q]], out=tt[q][:], in_=ps[pb][:], func=AF.Identity,
                     bias=gb2[:, c2:c2 + 1], scale=gate_mix1(c2))
                S.op("dve", "tensor_tensor", [b_tt[q], b_h[c2][tg]], [b_h[c2][tg]], out=hT[:, c2, c0:c0 + 512],
                     in0=tt[q][:], in1=hT[:, c2, c0:c0 + 512], op=ALU.add)
            ring_release(ws)
    bar = S.barrier()
    AR.lo, AR.hi = G_LO, G_HI
    if stop == "l1mix":
        final(False)
        return finish()
    bar = lat_ffn(3, 1, 2)
    if stop == "l1f1":
        final(False)
        return finish()
    final(True)
    return finish()


def _feat(v):
    v = np.asarray(v, np.float32).reshape(-1)
    return np.ascontiguousarray(v.reshape(-1, 128).T)


def rope_tables():
    t = np.arange(2048)
    row = (t // 64).astype(np.float32)
    col = (t % 64).astype(np.float32)
    fr = (np.float32(10000.0) ** (-np.arange(16, dtype=np.float32) / np.float32(16))).astype(np.float32)
    ang = np.concatenate([row[:, None] * fr, col[:, None] * fr], -1).astype(np.float32)
    cos = np.repeat(np.cos(ang).astype(np.float32), 2, axis=1).T
    sin = np.repeat(np.sin(ang).astype(np.float32), 2, axis=1).T
    sgn = np.where(np.arange(64) % 2 == 0, -1.0, 1.0).astype(np.float32)[:, None]
    return np.ascontiguousarray(cos), np.ascontiguousarray(sin * sgn)


def make_in_maps(inp, ncores=8, batches=None):
    I = {k: np.asarray(v) for k, v in inp.items()}
    JB = 288 // ncores
    SO = small_layout(JB)
    cosT, sinT = rope_tables()
    ident = np.eye(128, dtype=np.float32)
    swap = np.arange(64) ^ 1
    wuq = I["mla_wuq"][0].reshape(512, NH, 192)
    wuq_ext = np.concatenate([wuq, wuq[:, :, 128:][:, :, swap]], axis=2).reshape(512, NH * 256)
    wdkv = I["mla_wdkv"][0]
    wdkv_ext = np.concatenate([wdkv, wdkv[:, 512:][:, swap]], axis=1)
    wukv = I["mla_wukv"][0].reshape(512, 4, 4, 2, 128)
    wukv_r = np.ascontiguousarray(wukv.transpose(0, 1, 3, 2, 4)).reshape(512, 4096)
    ffn_w1 = I["ffn_w1"].reshape(4, D, DFF); ffn_w3 = I["ffn_w3"].reshape(4, D, DFF)
    ffn_w2 = I["ffn_w2"].reshape(4, DFF, D)
    adaw_flat = I["ada_w"]
    adab_flat = I["ada_b"].reshape(-1)
    common = dict(ident=ident, ffn_w1=ffn_w1, ffn_w3=ffn_w3, ffn_w2=ffn_w2, wdq=I["mla_wdq"][0],
                  wuq=np.ascontiguousarray(wuq_ext), wdkv=np.ascontiguousarray(wdkv_ext), wukv=wukv_r,
                  wo=I["mla_wo"][0], cw1=I["conv_w1"][0], cw2=I["conv_w2"][0])
    sm_base = np.zeros((128, NSM), np.float32)
    sm_base[:, SO["normg"]:SO["normg"] + 96] = _feat(I["norm_g"].reshape(-1))
    sm_base[:, SO["finalg"]:SO["finalg"] + 16] = _feat(I["final_g"])
    sm_base[:, SO["gq"]:SO["gq"] + 4] = _feat(I["mla_gq"][0])
    sm_base[:, SO["gkv"]:SO["gkv"] + 4] = _feat(I["mla_gkv"][0])
    sm_base[:, SO["b1"]:SO["b1"] + 32] = _feat(I["conv_b1"][0])
    dw = I["conv_dw"][0]
    sm_base[:, SO["dw"]:SO["dw"] + 16 * CW] = dw.T.reshape(16, 128, CW).transpose(1, 0, 2).reshape(128, 16 * CW)
    sm_base[:, SO["dwb"]:SO["dwb"] + 16] = _feat(I["conv_dwb"][0])
    sm_base[:, SO["lng"]:SO["lng"] + 16] = _feat(I["conv_ln_g"][0])
    sm_base[:, SO["lnb"]:SO["lnb"] + 16] = _feat(I["conv_ln_b"][0])
    sm_base[:, SO["b2"]:SO["b2"] + 16] = _feat(I["conv_b2"][0])
    per_layer = 144 // JB if JB <= 144 else 1
    in_maps = []
    for core in range(ncores):
        b = (core // 2) if batches is None else batches[core // 2]
        half = core % 2
        sm = sm_base.copy()
        sel = np.zeros(16, np.float32)
        sel[b] = 1.0
        if half == 1:
            sel[5] = 1.0
        else:
            sel[8] = 1.0
        sm[:, SO["sel"]:SO["sel"] + 16] = sel[None, :]
        g0 = core * JB
        sm[:, SO["adab"]:SO["adab"] + JB] = _feat(adab_flat[g0 * 128:(g0 + JB) * 128])
        cond = np.concatenate([I["c"], I["c_ctx"][None, :]], 0)
        sm[:, SO["cond"]:SO["cond"] + 80] = cond.T.reshape(16, 128, 5).transpose(1, 0, 2).reshape(128, 80)
        lay, c0 = divmod(g0 * 128, 18432)
        m = dict(common)
        m["adaw"] = np.ascontiguousarray(adaw_flat[lay][:, c0:c0 + JB * 128])
        m["x"] = np.ascontiguousarray(I["x"][b, half * T:(half + 1) * T])
        m["ctx"] = np.ascontiguousarray(I["ctx"][b, half * TC:(half + 1) * TC])
        m["smalls"] = sm
        m["rope"] = np.ascontiguousarray(np.stack([cosT[:, half * T:(half + 1) * T], sinT[:, half * T:(half + 1) * T]], 1))
        in_maps.append(m)
    return in_maps


_NC_CACHE = {}


def kernel(**inputs):
    ncores = 8
    if "nc" not in _NC_CACHE:
        _NC_CACHE["nc"] = build(ncores, "final")
    nc = _NC_CACHE["nc"]
    in_maps = make_in_maps(inputs, ncores)
    res = run_bass_kernel_spmd(nc, in_maps, core_ids=list(range(ncores)))
    out = np.empty((4, 2048, D), np.float32)
    for core in range(ncores):
        out[core // 2, (core % 2) * T:(core % 2 + 1) * T] = res.results[core]["out"]
    return out
```
